# Optimizing a Trainium2 kernel written in Bass

```python
import math
import jax, jax.numpy as jnp
from jax import lax
import numpy as np

D_MODEL = 2048
BATCH = 2
SEQ = 4096
DEPTH = 4

CHUNK = 64
N_MIXERS = 2
S5_WIDTH = D_MODEL
S5_GROUP = 16
S5_GROUPS = S5_WIDTH // S5_GROUP
S5_STATE = 64
DT_MIN = 0.001
DT_MAX = 0.1
EIG_CLIP = -1e-4
SGU_EXPAND = 6
SGU_HALF = SGU_EXPAND * D_MODEL // 2
SGU_BLOCK = 128
SGU_HEADS = 16
SGU_HEAD_DIM = SGU_HALF // SGU_HEADS
FFN_HIDDEN = 4 * D_MODEL
EPS = 1e-6

kernel_name = "hybrid_s5_sgu_streaming_encoder"


def rms_norm(x, g):
    xf = x.astype(jnp.float32)
    y = xf * lax.rsqrt(jnp.mean(xf * xf, axis=-1, keepdims=True) + EPS)
    return (y * g.astype(jnp.float32)).astype(x.dtype)


def layer_norm(x, g, b):
    xf = x.astype(jnp.float32)
    mu = jnp.mean(xf, axis=-1, keepdims=True)
    xc = xf - mu
    y = xc * lax.rsqrt(jnp.mean(xc * xc, axis=-1, keepdims=True) + EPS)
    return (y * g.astype(jnp.float32) + b.astype(jnp.float32)).astype(x.dtype)


def _linear_recurrence_op(e1, e2):
    a1r, a1i, b1r, b1i = e1
    a2r, a2i, b2r, b2i = e2
    return (a2r * a1r - a2i * a1i,
            a2r * a1i + a2i * a1r,
            a2r * b1r - a2i * b1i + b2r,
            a2r * b1i + a2i * b1r + b2i)


def s5_mixer(h, w_in, a_re, a_im, log_dt, b_re, b_im, c_re, c_im, d_skip, w_glu, w_out):
    f32 = jnp.float32
    bsz, seq, _ = h.shape
    u = (h @ w_in).astype(f32)
    u_g = u.reshape(bsz, seq, S5_GROUPS, S5_GROUP)
    lam_re = jnp.minimum(a_re.astype(f32), EIG_CLIP)
    lam_im = a_im.astype(f32)
    dt = jnp.exp(log_dt.astype(f32))[:, None]
    mag = jnp.exp(lam_re * dt)
    ab_re = mag * jnp.cos(lam_im * dt)
    ab_im = mag * jnp.sin(lam_im * dt)
    denom = lam_re * lam_re + lam_im * lam_im
    coef_re = ((ab_re - 1.0) * lam_re + ab_im * lam_im) / denom
    coef_im = (ab_im * lam_re - (ab_re - 1.0) * lam_im) / denom
    br = b_re.astype(f32)
    bi = b_im.astype(f32)
    bb_re = coef_re[..., None] * br - coef_im[..., None] * bi
    bb_im = coef_re[..., None] * bi + coef_im[..., None] * br
    bu_re = jnp.einsum('blgc,gpc->blgp', u_g, bb_re)
    bu_im = jnp.einsum('blgc,gpc->blgp', u_g, bb_im)
    a_seq_re = jnp.broadcast_to(ab_re, bu_re.shape)
    a_seq_im = jnp.broadcast_to(ab_im, bu_im.shape)
    _, _, s_re, s_im = lax.associative_scan(
        _linear_recurrence_op, (a_seq_re, a_seq_im, bu_re, bu_im), axis=1)
    y = (jnp.einsum('blgp,gcp->blgc', s_re, c_re.astype(f32))
         - jnp.einsum('blgp,gcp->blgc', s_im, c_im.astype(f32)))
    y = y.reshape(bsz, seq, S5_WIDTH) + d_skip.astype(f32) * u
    y = y.astype(h.dtype)
    z = jax.nn.gelu(y)
    z = z * jax.nn.sigmoid(z @ w_glu)
    return z @ w_out


def sgu_mixer(h, w_in, ln_g, ln_b, w_s, b_s, w_out):
    bsz, seq, _ = h.shape
    z = jax.nn.gelu(h @ w_in)
    u, v = jnp.split(z, 2, axis=-1)
    v = layer_norm(v, ln_g, ln_b)
    n_blk = seq // SGU_BLOCK
    v = v.reshape(bsz, n_blk, SGU_BLOCK, SGU_HEADS, SGU_HEAD_DIM)
    pos = jnp.arange(SGU_BLOCK)
    mask = (pos[None, :] // CHUNK) <= (pos[:, None] // CHUNK)
    w = jnp.where(mask[None], w_s, jnp.zeros((), w_s.dtype))
    s = jnp.einsum('hts,bnshd->bnthd', w, v) + b_s.T[None, None, :, :, None]
    s = s.reshape(bsz, seq, SGU_HALF)
    return (u * s) @ w_out


def squared_relu_mlp(h, w_up, w_down):
    a = jax.nn.relu(h @ w_up)
    return (a * a) @ w_down


def setup_inputs(seed: int = 0) -> dict:
    key = jax.random.key(seed)
    ks = jax.random.split(key, 24)
    n_a = (DEPTH + 1) // 2
    n_b = DEPTH // 2
    nrm = jax.random.normal
    f32 = jnp.float32
    x = nrm(ks[0], (BATCH, SEQ, D_MODEL), f32)
    norm_g = 1.0 + 0.02 * nrm(ks[1], (DEPTH, 4, D_MODEL), f32)
    s5_w_in = nrm(ks[2], (n_a, D_MODEL, S5_WIDTH), f32) * D_MODEL ** -0.5
    s5_a_re = -0.5 + 0.01 * nrm(ks[3], (n_a, S5_GROUPS, S5_STATE), f32)
    s5_a_im = (jnp.pi * jnp.arange(S5_STATE, dtype=f32))[None, None, :] + 0.01 * nrm(ks[4], (n_a, S5_GROUPS, S5_STATE), f32)
    s5_log_dt = jax.random.uniform(ks[5], (n_a, S5_GROUPS), f32, minval=math.log(DT_MIN), maxval=math.log(DT_MAX))
    b_scale = (2.0 * S5_GROUP) ** -0.5
    c_scale = (2.0 * S5_STATE) ** -0.5
    s5_b_re = nrm(ks[6], (n_a, S5_GROUPS, S5_STATE, S5_GROUP), f32) * b_scale
    s5_b_im = nrm(ks[7], (n_a, S5_GROUPS, S5_STATE, S5_GROUP), f32) * b_scale
    s5_c_re = nrm(ks[8], (n_a, S5_GROUPS, S5_GROUP, S5_STATE), f32) * c_scale
    s5_c_im = nrm(ks[9], (n_a, S5_GROUPS, S5_GROUP, S5_STATE), f32) * c_scale
    s5_d = nrm(ks[10], (n_a, S5_WIDTH), f32)
    s5_w_glu = nrm(ks[11], (n_a, S5_WIDTH, S5_WIDTH), f32) * S5_WIDTH ** -0.5
    s5_w_out = nrm(ks[12], (n_a, S5_WIDTH, D_MODEL), f32) * S5_WIDTH ** -0.5
    sgu_w_in = nrm(ks[13], (n_b, D_MODEL, 2 * SGU_HALF), f32) * D_MODEL ** -0.5
    sgu_ln_g = 1.0 + 0.02 * nrm(ks[14], (n_b, SGU_HALF), f32)
    sgu_ln_b = 0.02 * nrm(ks[15], (n_b, SGU_HALF), f32)
    sgu_w_s = nrm(ks[16], (n_b, SGU_HEADS, SGU_BLOCK, SGU_BLOCK), f32) * (0.5 * SGU_BLOCK ** -0.5)
    sgu_b_s = 1.0 + 0.1 * nrm(ks[17], (n_b, SGU_HEADS, SGU_BLOCK), f32)
    sgu_w_out = nrm(ks[18], (n_b, SGU_HALF, D_MODEL), f32) * SGU_HALF ** -0.5
    ffn_w_up = nrm(ks[19], (DEPTH, D_MODEL, FFN_HIDDEN), f32) * D_MODEL ** -0.5
    ffn_w_down = nrm(ks[20], (DEPTH, FFN_HIDDEN, D_MODEL), f32) * FFN_HIDDEN ** -0.5
    return {"x": x, "norm_g": norm_g,
            "s5_w_in": s5_w_in, "s5_a_re": s5_a_re, "s5_a_im": s5_a_im, "s5_log_dt": s5_log_dt,
            "s5_b_re": s5_b_re, "s5_b_im": s5_b_im, "s5_c_re": s5_c_re, "s5_c_im": s5_c_im,
            "s5_d": s5_d, "s5_w_glu": s5_w_glu, "s5_w_out": s5_w_out,
            "sgu_w_in": sgu_w_in, "sgu_ln_g": sgu_ln_g, "sgu_ln_b": sgu_ln_b,
            "sgu_w_s": sgu_w_s, "sgu_b_s": sgu_b_s, "sgu_w_out": sgu_w_out,
            "ffn_w_up": ffn_w_up, "ffn_w_down": ffn_w_down}


def reference(x, norm_g, s5_w_in, s5_a_re, s5_a_im, s5_log_dt, s5_b_re, s5_b_im, s5_c_re, s5_c_im,
              s5_d, s5_w_glu, s5_w_out, sgu_w_in, sgu_ln_g, sgu_ln_b, sgu_w_s, sgu_b_s, sgu_w_out,
              ffn_w_up, ffn_w_down):
    h = x
    for i in range(DEPTH):
        g = norm_g[i]
        hn = rms_norm(h, g[0])
        j = i // N_MIXERS
        if i % N_MIXERS == 0:
            m = s5_mixer(hn, s5_w_in[j], s5_a_re[j], s5_a_im[j], s5_log_dt[j], s5_b_re[j], s5_b_im[j],
                         s5_c_re[j], s5_c_im[j], s5_d[j], s5_w_glu[j], s5_w_out[j])
        else:
            m = sgu_mixer(hn, sgu_w_in[j], sgu_ln_g[j], sgu_ln_b[j], sgu_w_s[j], sgu_b_s[j], sgu_w_out[j])
        h = h + rms_norm(m, g[1])
        f = squared_relu_mlp(rms_norm(h, g[2]), ffn_w_up[i], ffn_w_down[i])
        h = h + rms_norm(f, g[3])
    return h
```

```python
from contextlib import ExitStack
import math
import numpy as np
import concourse.bass as bass
import concourse.mybir as mybir
from concourse.bass_utils import run_bass_kernel_spmd

F32 = mybir.dt.float32
BF16 = mybir.dt.bfloat16
I32 = mybir.dt.int32
AF = mybir.ActivationFunctionType
ALU = mybir.AluOpType
AX = mybir.AxisListType

PE, ACT, DVE, POOL, SP = "tensor", "scalar", "vector", "gpsimd", "sync"
ENGS = (PE, ACT, DVE, POOL, SP)
DMA_RING = 8

D = 2048
T = 1024
NCORE = 8
DEPTH = 4
FF = 8192
SH = 6144
EPS = 1e-6
EIG_CLIP = -1e-4


class Op:
    __slots__ = ("eng", "fn", "reads", "writes", "is_dma", "idx", "deps", "signal",
                 "sig_val", "dma_sem", "dma_val", "nobar", "barrier")

    def __init__(self, eng, fn, reads, writes, is_dma, nobar=False):
        self.eng = eng
        self.fn = fn
        self.reads = reads
        self.writes = writes
        self.is_dma = is_dma
        self.deps = []
        self.signal = False
        self.sig_val = 0
        self.dma_sem = None
        self.dma_val = 0
        self.nobar = nobar
        self.barrier = False


class Prog:
    def __init__(self, nc):
        self.nc = nc
        self.ops = []
        self.engs = {PE: nc.tensor, ACT: nc.scalar, DVE: nc.vector, POOL: nc.gpsimd, SP: nc.sync}

    def op(self, eng, fn, reads=(), writes=(), dma=False, nobar=False):
        o = Op(eng, fn, tuple(reads), tuple(writes), dma, nobar)
        o.idx = len(self.ops)
        self.ops.append(o)
        return o

    def dma(self, eng, out, in_, reads=(), writes=(), nobar=False, **kw):
        return self.op(eng, lambda e: e.dma_start(out=out, in_=in_, **kw), reads, writes,
                       dma=True, nobar=nobar)

    def barrier(self):
        for en in (PE, ACT, DVE, SP, POOL):
            o = Op(en, None, (), (), False)
            o.barrier = True
            o.idx = len(self.ops)
            self.ops.append(o)

    def finalize(self, stack):
        nc = self.nc
        ops = self.ops
        last_w = {}
        readers = {}
        last_of_eng = {}
        pending_dma = []
        i = 0
        n = len(ops)
        while i < n:
            o = ops[i]
            if o.barrier:
                j = i
                while j < n and ops[j].barrier:
                    j += 1
                deps = [v for v in last_of_eng.values()]
                deps += [d for d in pending_dma if not ops[d].nobar]
                for b in range(i, j):
                    ops[b].deps = sorted(set(deps))
                    for d in ops[b].deps:
                        ops[d].signal = True
                pending_dma = [d for d in pending_dma if ops[d].nobar]
                i = j
                continue
            deps = set()
            for k in o.reads:
                w = last_w.get(k)
                if w is not None:
                    deps.add(w)
            for k in o.writes:
                w = last_w.get(k)
                if w is not None:
                    deps.add(w)
                for r in readers.get(k, ()):
                    deps.add(r)
            deps.discard(o.idx)
            dl = []
            for d in deps:
                od = ops[d]
                if od.eng == PE and o.eng == PE and not od.is_dma and not o.is_dma:
                    continue
                dl.append(d)
            o.deps = sorted(dl)
            for d in o.deps:
                ops[d].signal = True
            for k in o.reads:
                rl = readers.setdefault(k, [])
                if not o.is_dma:
                    rl[:] = [r for r in rl if ops[r].is_dma or ops[r].eng != o.eng]
                rl.append(o.idx)
            for k in o.writes:
                last_w[k] = o.idx
                readers[k] = []
            if o.is_dma:
                pending_dma.append(o.idx)
            else:
                last_of_eng[o.eng] = o.idx
            i += 1
        eng_sems = {}
        dma_rings = {}
        eng_cnt = {}
        dma_cnt = {}
        for e in self.engs:
            eng_sems[e] = stack.enter_context(nc.semaphore("s_" + e))
            eng_cnt[e] = 0
            dma_rings[e] = None
            dma_cnt[e] = 0
        for o in ops:
            if o.barrier:
                continue
            if o.is_dma:
                if dma_rings[o.eng] is None:
                    dma_rings[o.eng] = [stack.enter_context(nc.semaphore("d_%s_%d" % (o.eng, i)))
                                        for i in range(DMA_RING)]
                c = dma_cnt[o.eng]
                dma_cnt[o.eng] = c + 1
                o.dma_sem = (o.eng, c % DMA_RING)
                o.dma_val = 16 * (c // DMA_RING + 1)
            elif o.signal:
                eng_cnt[o.eng] += 1
                o.sig_val = eng_cnt[o.eng]
        waited = {e: {} for e in self.engs}
        n_wait = 0
        dma_seq = {e: 0 for e in self.engs}
        for o in ops:
            e = self.engs[o.eng]
            need = {}
            for d in o.deps:
                od = ops[d]
                if od.is_dma:
                    key = ("d",) + od.dma_sem
                    val = od.dma_val
                else:
                    key = ("e", od.eng)
                    val = od.sig_val
                if need.get(key, 0) < val:
                    need[key] = val
            if o.is_dma:
                c = dma_seq[o.eng]
                dma_seq[o.eng] = c + 1
                if c >= DMA_RING:
                    key = ("d", o.eng, c % DMA_RING)
                    val = 16 * (c // DMA_RING)
                    if need.get(key, 0) < val:
                        need[key] = val
            for key, val in need.items():
                if waited[o.eng].get(key, 0) >= val:
                    continue
                waited[o.eng][key] = val
                sem = dma_rings[key[1]][key[2]] if key[0] == "d" else eng_sems[key[1]]
                e.wait_ge(sem, val)
                n_wait += 1
            if o.barrier:
                continue
            ins = o.fn(e)
            if o.is_dma:
                ins.then_inc(dma_rings[o.dma_sem[0]][o.dma_sem[1]], 16)
            elif o.signal:
                ins.then_inc(eng_sems[o.eng], 1)
        for en, ring in dma_rings.items():
            if ring is None:
                continue
            c = dma_cnt[en]
            for i in range(min(c, DMA_RING)):
                cnt = (c - 1 - i) // DMA_RING + 1
                self.engs[en].wait_ge(ring[i], 16 * cnt)
        self.stats = dict(n_ops=len(ops), n_wait=n_wait, sig=dict(eng_cnt), dma=dict(dma_cnt))
        return self.stats


class Ctx:
    pass


W_SLOT_ELEMS = 8192
N_WSLOT = 3


class WStream:
    def __init__(self, P, slots):
        self.P = P
        self.slots = slots
        self.plan = []
        self.issued = 0
        self.next_use = 0

    def add(self, src_ap, shape, pitch=None):
        self.plan.append((src_ap, shape, pitch))

    def _view(self, i):
        src, shape, pitch = self.plan[i]
        s = i % N_WSLOT
        if len(shape) == 3:
            b = pitch or shape[2]
            view = self.slots[s][:, 0:shape[1] * b].rearrange("p (a b) -> p a b", a=shape[1])
            if b != shape[2]:
                view = view[:, :, 0:shape[2]]
        else:
            view = self.slots[s][:, 0:shape[1]]
        return view, s

    def _issue(self, i):
        view, s = self._view(i)
        self.P.dma(POOL, view, self.plan[i][0], writes=[("wslot", s)], nobar=True)

    def take(self):
        i = self.next_use
        self.next_use += 1
        while self.issued < min(len(self.plan), i + N_WSLOT):
            self._issue(self.issued)
            self.issued += 1
        view, s = self._view(i)
        return view, ("wslot", s)


def tile_rows(h_ap, blk, b):
    if blk == "nat":
        return h_ap[b * 128:(b + 1) * 128, :]
    return h_ap.rearrange("(j t) d -> t j d", t=8)[b]


def hnT_dst(hnT, k0, k1, src_blk, dst_ord, b):
    nk = k1 - k0
    if src_blk == dst_ord:
        return hnT[:, k0:k1, b * 128:(b + 1) * 128], None
    if src_blk == "str" and dst_ord == "nat":
        v = hnT[:, k0:k1, :].rearrange("p k (j t) -> p k t j", t=8)[:, :, b, :]
        return v, None
    v = hnT[:, k0:k1, :].rearrange("p k (t j) -> p k t j", t=8)[:, :, :, 16 * b:16 * b + 16]
    return v, "p (k j t) -> p k t j"


def emit_boundary(C, prev, nxt):
    P, nc = C.P, C.nc
    blk = prev["ord"] if prev is not None else nxt["ord"]
    ht = [C.view(C.SCR + i * 8192, [128, D], F32) for i in range(2)]
    hb = [C.view(C.SCR + 16384 + i * 4096, [128, D], BF16) for i in range(2)]
    gpost = C.view(C.SCR + 24576, [128, D], F32)
    gpre = C.view(C.SCR + 32768, [128, D], F32)
    junk = C.view(C.SCR + 40960, [128, D], F32)
    st = C.view(C.SCR + 49152, [128, 64], F32)
    if prev is not None:
        l, i = prev["gpost"]
        P.dma(SP, gpost, C.norm_g[l, i, :].partition_broadcast(128), writes=["gpost"])
    if nxt is not None:
        l, i = nxt["gpre"]
        P.dma(SP, gpre, C.norm_g[l, i, :].partition_broadcast(128), writes=["gpre"])
    for b in range(8):
        s = b % 2
        h = ht[s]
        src = C.h_in if prev is None else C.h_out
        P.dma(SP, h, tile_rows(src, blk, b), reads=["hpark"], writes=[("ht", s)])
        c0 = b * 8
        if prev is not None:
            m = C.m_acc[:, b, :]
            P.op(ACT, lambda e, m=m, c0=c0: e.activation(out=junk, in_=m, func=AF.Square,
                                                         accum_out=st[:, c0:c0 + 1]),
                 reads=[("macc", b)], writes=["junk", ("st", b)])
            P.op(ACT, lambda e, c0=c0: e.activation(out=st[:, c0 + 1:c0 + 2], in_=st[:, c0:c0 + 1],
                                                    func=AF.Sqrt, scale=1.0 / D, bias=C.eps_t[:, 0:1]),
                 reads=[("st", b)], writes=[("st", b)])
            P.op(DVE, lambda e, c0=c0: e.reciprocal(out=st[:, c0 + 2:c0 + 3], in_=st[:, c0 + 1:c0 + 2]),
                 reads=[("st", b)], writes=[("st", b)])
            P.op(DVE, lambda e, m=m, c0=c0: e.scalar_tensor_tensor(
                out=m, in0=m, scalar=st[:, c0 + 2:c0 + 3], op0=ALU.mult, in1=gpost, op1=ALU.mult),
                reads=[("st", b), "gpost"], writes=[("macc", b)])
            P.op(DVE, lambda e, m=m, h=h: e.tensor_tensor(out=h, in0=h, in1=m, op=ALU.add),
                 reads=[("macc", b)], writes=[("ht", s)])
            dst = C.h_out
            P.dma(SP, tile_rows(dst, blk, b), h, reads=[("ht", s)], writes=["hpark"])
        elif nxt is not None:
            P.dma(SP, tile_rows(C.h_out, blk, b), h, reads=[("ht", s)], writes=["hpark"])
        if nxt is None:
            continue
        P.op(ACT, lambda e, h=h, c0=c0: e.activation(out=junk, in_=h, func=AF.Square,
                                                     accum_out=st[:, c0 + 3:c0 + 4]),
             reads=[("ht", s)], writes=["junk", ("st", b)])
        P.op(ACT, lambda e, c0=c0: e.activation(out=st[:, c0 + 4:c0 + 5], in_=st[:, c0 + 3:c0 + 4],
                                                func=AF.Sqrt, scale=1.0 / D, bias=C.eps_t[:, 0:1]),
             reads=[("st", b)], writes=[("st", b)])
        P.op(DVE, lambda e, c0=c0: e.reciprocal(out=st[:, c0 + 5:c0 + 6], in_=st[:, c0 + 4:c0 + 5]),
             reads=[("st", b)], writes=[("st", b)])
        P.op(DVE, lambda e, h=h, s=s, c0=c0: e.scalar_tensor_tensor(
            out=hb[s], in0=h, scalar=st[:, c0 + 5:c0 + 6], op0=ALU.mult, in1=gpre, op1=ALU.mult),
            reads=[("ht", s), ("st", b), "gpre"], writes=[("hb", s)])
        for half in range(2):
            pb = half
            for kk in range(8):
                k = half * 8 + kk
                P.op(PE, lambda e, s=s, k=k, kk=kk, pb=pb: e.transpose(
                    out=C.psb[pb][:, kk * 128:(kk + 1) * 128], in_=hb[s][:, k * 128:(k + 1) * 128],
                    identity=C.ident), reads=[("hb", s), "ident"], writes=[("ps", pb)])
            dst, rr = hnT_dst(C.hnT, half * 8, half * 8 + 8, blk, nxt["ord"], b)
            if rr is None:
                srcv = C.psb[pb].rearrange("p (k i) -> p k i", k=8)
            else:
                srcv = C.psb[pb].rearrange(rr, k=8, t=8)
            P.op(ACT, lambda e, dst=dst, srcv=srcv: e.copy(out=dst, in_=srcv),
                 reads=[("ps", pb)], writes=["hnT"])


def plan_ffn(C, ph):
    l = ph["layer"]
    wu = C.w["ffn_w_up"][l].rearrange("(kc p) n -> p kc n", p=128)
    wd = C.w["ffn_w_down"][l].rearrange("(kc p) n -> p kc n", p=128)
    for grp in range(8):
        for hg in range(2):
            c0 = grp * 1024 + hg * 512
            C.ws.add(wu[:, :, c0:c0 + 512], [128, 16, 512])
        for ch in range(2):
            C.ws.add(wd[:, grp * 8:(grp + 1) * 8, ch * 1024:(ch + 1) * 1024], [128, 8, 1024])


def emit_ffn(C, ph):
    P = C.P
    aT = [C.view(C.SCR + i * 16384, [128, 8, T], BF16) for i in range(2)]
    rt = [C.view(C.SCR + 32768 + i * 2048, [128, 512], F32) for i in range(2)]
    cnt = 0
    for grp in range(8):
        a = aT[grp % 2]
        akey = ("aT", grp % 2)
        for hg in range(2):
            wv, wkey = C.ws.take()
            for jj in range(4):
                j = hg * 4 + jj
                for half in range(2):
                    pb = 2 + cnt % 2
                    r = rt[cnt % 2]
                    rkey = ("rt", cnt % 2)
                    cnt += 1
                    for k in range(16):
                        P.op(PE, lambda e, pb=pb, wv=wv, k=k, jj=jj, half=half: e.matmul(
                            C.ps[pb], lhsT=wv[:, k, jj * 128:(jj + 1) * 128],
                            rhs=C.hnT[:, k, half * 512:(half + 1) * 512], start=(k == 0), stop=(k == 15)),
                            reads=[wkey, "hnT"], writes=[("ps", pb)])
                    P.op(ACT, lambda e, pb=pb, r=r: e.activation(out=r, in_=C.ps[pb], func=AF.Relu),
                         reads=[("ps", pb)], writes=[rkey])
                    P.op(DVE, lambda e, r=r, a=a, j=j, half=half: e.tensor_tensor(
                        out=a[:, j, half * 512:(half + 1) * 512], in0=r, in1=r, op=ALU.mult),
                        reads=[rkey], writes=[akey])
        for ch in range(2):
            wv, wkey = C.ws.take()
            for tb in range(8):
                for nn in range(2):
                    pb = 4 + cnt % 4
                    cnt += 1
                    for j in range(8):
                        P.op(PE, lambda e, pb=pb, wv=wv, j=j, tb=tb, nn=nn, a=a: e.matmul(
                            C.ps[pb], lhsT=a[:, j, tb * 128:(tb + 1) * 128],
                            rhs=wv[:, j, nn * 512:(nn + 1) * 512], start=(j == 0), stop=(j == 7)),
                            reads=[wkey, akey], writes=[("ps", pb)])
                    mv = C.m_acc[:, tb, ch * 1024 + nn * 512: ch * 1024 + nn * 512 + 512]
                    mkey = ("macc", tb)
                    if grp == 0:
                        P.op(DVE, lambda e, mv=mv, pb=pb: e.tensor_copy(out=mv, in_=C.ps[pb]),
                             reads=[("ps", pb)], writes=[mkey])
                    else:
                        P.op(DVE, lambda e, mv=mv, pb=pb: e.tensor_tensor(out=mv, in0=C.ps[pb], in1=mv,
                                                                          op=ALU.add),
                             reads=[("ps", pb), mkey], writes=[mkey])


GELU_C = 1.5957691216057308


def emit_gelu(C, out, x_ps, tmp, pkey, tkey, okeys, extra_reads=()):
    P = C.P
    if True:
        P.op(ACT, lambda e: e.activation(out=out, in_=x_ps, func=AF.Gelu_apprx_tanh),
             reads=[pkey] + list(extra_reads), writes=list(okeys))
        return
    P.op(ACT, lambda e: e.activation(out=tmp, in_=x_ps, func=AF.Square), reads=[pkey], writes=[tkey])
    P.op(DVE, lambda e: e.tensor_scalar(out=tmp, in0=tmp, scalar1=0.044715, scalar2=1.0,
                                        op0=ALU.mult, op1=ALU.add), reads=[tkey], writes=[tkey])
    P.op(DVE, lambda e: e.tensor_tensor(out=tmp, in0=tmp, in1=x_ps, op=ALU.mult),
         reads=[tkey, pkey], writes=[tkey])
    P.op(ACT, lambda e: e.activation(out=tmp, in_=tmp, func=AF.Sigmoid, scale=GELU_C),
         reads=[tkey], writes=[tkey])
    P.op(DVE, lambda e: e.tensor_tensor(out=out, in0=tmp, in1=x_ps, op=ALU.mult),
         reads=[tkey, pkey] + list(extra_reads), writes=list(okeys))


def plan_sgu(C, ph):
    j = ph["layer"] // 2
    wi = C.w["sgu_w_in"][j].rearrange("(kc p) n -> p kc n", p=128)
    wo = C.w["sgu_w_out"][j].rearrange("(kc p) n -> p kc n", p=128)
    for ct in range(12):
        C.ws.add(wi[:, :, SH + ct * 512: SH + (ct + 1) * 512], [128, 16, 512])
    for hd in range(16):
        C.ws.add(wi[:, :, hd * 384:(hd + 1) * 384], [128, 16, 384], pitch=512)
        C.ws.add(wo[:, hd * 3:(hd + 1) * 3, :], [128, 3, D])


def emit_sgu(C, ph):
    P = C.P
    j = ph["layer"] // 2
    S = C.SCR
    vh = [C.view(S + i * 1536, [128, 384], F32) for i in range(2)]
    uT = C.view(S + 3072, [128, 3, T], F32)
    vhb = C.view(S + 15360, [128, 8, 384], BF16)
    suT = C.view(S + 21504, [128, 3, T], BF16)
    s_tmp = C.view(S + 27648, [128, T], F32)
    rowsum = C.view(S + 31744, [128, 16, 128], F32)
    bsbc = C.view(S + 39936, [128, 16, 128], F32)
    wsT = C.view(S + 48128, [128, 16, 128], BF16)
    gt = [C.view(S + 52224 + i * 2048, [128, 512], F32) for i in range(2)]
    stats = C.view(S + 56320, [128, 8, 72], F32)
    sm = C.view(S + 58624, [128, 136], F32)
    mv = sm[:, 0:16].rearrange("p (a b) -> p a b", a=8)
    rstd = sm[:, 16:24]
    lngT = sm[:, 24:72]
    lnbT = sm[:, 72:120]
    btile = [C.view(S + 59168 + i * 512, [128, 128], F32) for i in range(3)]
    wsrc = C.view(S + 3072, [128, 16, 128], F32)
    wsb = C.view(S + 15360, [128, 16, 128], BF16)
    lnsrc = C.view(S + 21504, [128, 256], F32)[0:48, :]
    ones = C.view(S + 27648, [128, 128], BF16)
    Vsp = C.Vsp
    P.dma(SP, wsrc, C.sm["sgu_w_s"][j].rearrange("h t s -> t h s"), writes=["wsrc"])
    P.dma(SP, bsbc.rearrange("p h t -> p (h t)"),
          C.sm["sgu_b_s"][j].rearrange("h t -> (h t)").partition_broadcast(128), writes=["bsbc"])
    P.dma(SP, lnsrc[:, 0:128], C.sm["sgu_ln_g"][j].rearrange("(c p) -> c p", p=128), writes=["lnsrc"])
    P.dma(SP, lnsrc[:, 128:256], C.sm["sgu_ln_b"][j].rearrange("(c p) -> c p", p=128), writes=["lnsrc"])
    P.op(DVE, lambda e: e.memset(wsrc[0:64, :, 64:128], 0.0), reads=["wsrc"], writes=["wsrc"])
    P.op(DVE, lambda e: e.tensor_copy(out=wsb, in_=wsrc), reads=["wsrc"], writes=["wsb"])
    P.op(DVE, lambda e: e.memset(ones, 1.0), writes=["ones"])
    for g4 in range(2):
        for hh in range(8):
            hd = g4 * 8 + hh
            P.op(PE, lambda e, hd=hd, hh=hh, g4=g4: e.transpose(
                out=C.psb[g4][:, hh * 128:(hh + 1) * 128], in_=wsb[:, hd, :], identity=C.ident),
                reads=["wsb", "ident"], writes=[("ps", g4)])
        P.op(ACT, lambda e, g4=g4: e.copy(out=wsT[:, g4 * 8:(g4 + 1) * 8, :].rearrange("p h t -> p (h t)"),
                                          in_=C.psb[g4]), reads=[("ps", g4)], writes=["wsT"])
    for q in range(4):
        P.op(PE, lambda e, q=q: e.matmul(C.ps[2 + q], lhsT=ones,
                                         rhs=wsT[:, q * 4:(q + 1) * 4, :].rearrange("p h t -> p (h t)"),
                                         start=True, stop=True),
             reads=["ones", "wsT"], writes=[("ps", 2 + q)])
        P.op(ACT, lambda e, q=q: e.copy(out=rowsum[:, q * 4:(q + 1) * 4, :].rearrange("p h t -> p (h t)"),
                                        in_=C.ps[2 + q]), reads=[("ps", 2 + q)], writes=["rowsum"])
    for q in range(2):
        P.op(PE, lambda e, q=q: e.transpose(out=C.ps[6 + q][:, 0:48], in_=lnsrc[:, q * 128:(q + 1) * 128],
                                            identity=C.identf[0:48, 0:48]),
             reads=["lnsrc", "identf"], writes=[("ps", 6 + q)])
        dstv = lngT if q == 0 else lnbT
        P.op(ACT, lambda e, q=q, dstv=dstv: e.copy(out=dstv, in_=C.ps[6 + q][:, 0:48]),
             reads=[("ps", 6 + q)], writes=["lnT"])
    P.barrier()
    cnt = 0
    for ct in range(12):
        wv, wkey = C.ws.take()
        for tb in range(8):
            pb = cnt % 4
            g = gt[cnt % 2]
            gkey = ("gt", cnt % 2)
            cnt += 1
            for k in range(16):
                P.op(PE, lambda e, pb=pb, wv=wv, k=k, tb=tb: e.matmul(
                    C.ps[pb], lhsT=C.hnT[:, k, tb * 128:(tb + 1) * 128], rhs=wv[:, k, :],
                    start=(k == 0), stop=(k == 15)), reads=[wkey, "hnT"], writes=[("ps", pb)])
            emit_gelu(C, g, C.ps[pb], g, ("ps", pb), gkey, [gkey])
            P.op(DVE, lambda e, g=g, tb=tb, ct=ct: e.bn_stats(out=stats[:, tb, ct * 6:(ct + 1) * 6], in_=g),
                 reads=[gkey], writes=[("stats", tb)])
            P.dma(SP, Vsp[tb * 128:(tb + 1) * 128, ct * 512:(ct + 1) * 512], g, reads=[gkey],
                  writes=["Vsp"])
    for tb in range(8):
        P.op(DVE, lambda e, tb=tb: e.bn_aggr(out=mv[:, tb, :], in_=stats[:, tb, :]),
             reads=[("stats", tb)], writes=["mv"])
    P.op(ACT, lambda e: e.activation(out=rstd, in_=mv[:, :, 1], func=AF.Sqrt, bias=C.eps_t[:, 0:1]),
         reads=["mv"], writes=["rstd"])
    P.op(DVE, lambda e: e.reciprocal(out=rstd, in_=rstd), reads=["rstd"], writes=["rstd"])
    P.barrier()
    C.dbg("sm", sm, [128, 136], F32, [])
    C.dbg("rowsum", rowsum.rearrange("p h t -> p (h t)"), [128, 2048], F32, [])
    C.dbg("wsT", wsT.rearrange("p h t -> p (h t)"), [128, 2048], BF16, [])
    P.barrier()
    Vv = Vsp.rearrange("(tb p) f -> tb p f", p=128)
    vcnt = 0
    for hd in range(16):
        wu, wukey = C.ws.take()
        for tb in range(8):
            v = vh[vcnt % 2]
            vkey = ("vh", vcnt % 2)
            vcnt += 1
            P.dma(SP, v, Vv[tb][:, hd * 384:(hd + 1) * 384], reads=["Vsp"], writes=[vkey])
            P.op(DVE, lambda e, v=v, tb=tb: e.tensor_scalar(
                out=vhb[:, tb, :], in0=v, scalar1=mv[:, tb, 0:1], scalar2=rstd[:, tb:tb + 1],
                op0=ALU.subtract, op1=ALU.mult), reads=[vkey, "mv", "rstd"], writes=["vhb"])
        for cc in range(3):
            for half in range(2):
                pb = cnt % 4
                g = gt[cnt % 2]
                gkey = ("gt", cnt % 2)
                cnt += 1
                for k in range(16):
                    P.op(PE, lambda e, pb=pb, wu=wu, k=k, cc=cc, half=half: e.matmul(
                        C.ps[pb], lhsT=wu[:, k, cc * 128:(cc + 1) * 128],
                        rhs=C.hnT[:, k, half * 512:(half + 1) * 512], start=(k == 0), stop=(k == 15)),
                        reads=[wukey, "hnT"], writes=[("ps", pb)])
                emit_gelu(C, uT[:, cc, half * 512:(half + 1) * 512], C.ps[pb], g, ("ps", pb), gkey,
                          [("uT", cc)])
        for cc in range(3):
            ch = hd * 3 + cc
            bt = btile[cc]
            P.op(DVE, lambda e, bt=bt, ch=ch, hd=hd: e.scalar_tensor_tensor(
                out=bt, in0=rowsum[:, hd, :], scalar=lnbT[:, ch:ch + 1], op0=ALU.mult,
                in1=bsbc[:, hd, :], op1=ALU.add), reads=["rowsum", "bsbc", "lnT"], writes=[("bt", cc)])
            pbs = (4 + 2 * (cc % 2), 5 + 2 * (cc % 2))
            for tb in range(8):
                pb = pbs[tb // 4]
                P.op(PE, lambda e, pb=pb, tb=tb, cc=cc, hd=hd: e.matmul(
                    C.ps[pb][:, (tb % 4) * 128:(tb % 4 + 1) * 128], lhsT=vhb[:, tb, cc * 128:(cc + 1) * 128],
                    rhs=wsT[:, hd, :], start=True, stop=True),
                    reads=["vhb", "wsT"], writes=[("ps", pb)])
            for tb in range(8):
                pb = pbs[tb // 4]
                P.op(DVE, lambda e, pb=pb, tb=tb, bt=bt, ch=ch: e.scalar_tensor_tensor(
                    out=s_tmp[:, tb * 128:(tb + 1) * 128], in0=C.ps[pb][:, (tb % 4) * 128:(tb % 4 + 1) * 128],
                    scalar=lngT[:, ch:ch + 1], op0=ALU.mult, in1=bt, op1=ALU.add),
                    reads=[("ps", pb), ("bt", cc), "lnT"], writes=["s_tmp"])
            P.op(DVE, lambda e, cc=cc: e.tensor_tensor(out=suT[:, cc, :], in0=s_tmp, in1=uT[:, cc, :],
                                                       op=ALU.mult),
                 reads=["s_tmp", ("uT", cc)], writes=["suT"])
        if hd == 0:
            P.barrier()
            C.dbg("uT", uT.rearrange("p c t -> p (c t)"), [128, 3 * T], F32, [])
            C.dbg("suT", suT.rearrange("p c t -> p (c t)"), [128, 3 * T], BF16, [])
            C.dbg("vhb", vhb.rearrange("p c t -> p (c t)"), [128, 8 * 384], BF16, [])
            P.barrier()
        wo, wokey = C.ws.take()
        for tb in range(8):
            for nt in range(4):
                pb = cnt % 4
                cnt += 1
                for cc in range(3):
                    P.op(PE, lambda e, pb=pb, tb=tb, nt=nt, cc=cc, wo=wo: e.matmul(
                        C.ps[pb], lhsT=suT[:, cc, tb * 128:(tb + 1) * 128],
                        rhs=wo[:, cc, nt * 512:(nt + 1) * 512], start=(cc == 0), stop=(cc == 2)),
                        reads=["suT", wokey], writes=[("ps", pb)])
                mvw = C.m_acc[:, tb, nt * 512:(nt + 1) * 512]
                mkey = ("macc", tb)
                if hd == 0:
                    P.op(DVE, lambda e, mvw=mvw, pb=pb: e.tensor_copy(out=mvw, in_=C.ps[pb]),
                         reads=[("ps", pb)], writes=[mkey])
                else:
                    P.op(DVE, lambda e, mvw=mvw, pb=pb: e.tensor_tensor(out=mvw, in0=C.ps[pb], in1=mvw,
                                                                        op=ALU.add),
                         reads=[("ps", pb), mkey], writes=[mkey])


TWO_PI = 2.0 * math.pi


def plan_s5(C, ph):
    jl = ph["layer"] // 2
    wi = C.w["s5_w_in"][jl].rearrange("(kc p) n -> p kc n", p=128)
    for ct in range(4):
        C.ws.add(wi[:, :, ct * 512:(ct + 1) * 512], [128, 16, 512])
    if ph["kind"] == "s5":
        wg = C.w["s5_w_glu"][jl].rearrange("(kc p) n -> p kc n", p=128)
        wo = C.w["s5_w_out"][jl].rearrange("(kc p) n -> p kc n", p=128)
        for ct in range(4):
            C.ws.add(wg[:, :, ct * 512:(ct + 1) * 512], [128, 16, 512])
        for ct in range(4):
            C.ws.add(wo[:, :, ct * 512:(ct + 1) * 512], [128, 16, 512])


def emit_s5(C, ph):
    P = C.P
    jl = ph["layer"] // 2
    full = ph["kind"] == "s5"
    S = C.SCR
    RM, RH = 0, 65536
    V = C.view
    tt = lambda o, a, b, op, r, w: P.op(DVE, lambda e: e.tensor_tensor(out=o, in0=a, in1=b, op=op), reads=r, writes=w)

    bre = V(RM + 0, [128, 64, 16], F32)
    bim = V(RM + 4096, [128, 64, 16], F32)
    cpre = V(RM + 8192, [128, 16, 64], F32)
    cpim = V(RM + 12288, [128, 16, 64], F32)
    bbre = V(RM + 16384, [128, 64, 16], F32)
    bbim = V(RM + 20480, [128, 64, 16], F32)
    Pw = V(RM + 24576, [128, 9, 2, 64], F32)
    Pn = V(RM + 29184, [128, 9, 2, 64], F32)
    smt = [V(RM + 33792 + i * 256, [128, 64], F32) for i in range(32)]
    tmpA = V(RM + 41984, [128, 4096], F32)
    dg = V(RM + 58368, [128, 16], F32)
    Drep = V(RM + 58432, [128, 128], F32)
    Pdup = V(RM + 58944, [128, 2, 128], F32)
    sl = [V(S + i * 16384, [128, 4096], F32) for i in range(2)]
    (are, aim, ldt, dtt, lre, xx, ang, mag, qq, kf, r1, mk, rs, rc, sn, cs, a1re, a1im, nre, den, rden,
     cre, cim, t1, t2, t3, e2, qi) = smt[:28]
    sm = C.sm
    K0 = ["p0"]
    P.dma(SP, are, sm["s5_a_re"][jl], writes=K0)
    P.dma(SP, aim, sm["s5_a_im"][jl], writes=K0)
    P.dma(SP, ldt[:, 0:1], sm["s5_log_dt"][jl].rearrange("(g o) -> g o", o=1), writes=K0)
    P.dma(SP, bre, sm["s5_b_re"][jl], writes=K0)
    P.dma(SP, bim, sm["s5_b_im"][jl], writes=K0)
    P.dma(SP, cpre, sm["s5_c_re"][jl], writes=K0)
    P.dma(SP, cpim, sm["s5_c_im"][jl], writes=K0)
    P.dma(SP, dg, sm["s5_d"][jl].rearrange("(g c) -> g c", c=16), writes=K0)

    def dv(fn):
        P.op(DVE, fn, reads=K0, writes=K0)

    def ac(fn):
        P.op(ACT, fn, reads=K0, writes=K0)
    ac(lambda e: e.activation(out=dtt[:, 0:1], in_=ldt[:, 0:1], func=AF.Exp))
    dv(lambda e: e.tensor_scalar(out=lre, in0=are, scalar1=EIG_CLIP, scalar2=None, op0=ALU.min))
    dv(lambda e: e.tensor_scalar(out=xx, in0=lre, scalar1=dtt[:, 0:1], scalar2=None, op0=ALU.mult))
    dv(lambda e: e.tensor_scalar(out=ang, in0=aim, scalar1=dtt[:, 0:1], scalar2=None, op0=ALU.mult))
    ac(lambda e: e.activation(out=mag, in_=xx, func=AF.Exp))
    dv(lambda e: e.tensor_scalar(out=qq, in0=ang, scalar1=1.0 / TWO_PI, scalar2=None, op0=ALU.mult))
    dv(lambda e: e.tensor_copy(out=qi.bitcast(I32), in_=qq))
    dv(lambda e: e.tensor_copy(out=kf, in_=qi.bitcast(I32)))
    dv(lambda e: e.scalar_tensor_tensor(out=r1, in0=kf, scalar=-TWO_PI, op0=ALU.mult, in1=ang, op1=ALU.add))

    def wrap(dst, src, shift):
        dv(lambda e: e.tensor_scalar(out=dst, in0=src, scalar1=shift, scalar2=None, op0=ALU.add))
        dv(lambda e: e.tensor_scalar(out=mk, in0=dst, scalar1=math.pi, scalar2=-TWO_PI, op0=ALU.is_gt, op1=ALU.mult))
        dv(lambda e: e.tensor_tensor(out=dst, in0=dst, in1=mk, op=ALU.add))
        dv(lambda e: e.tensor_scalar(out=mk, in0=dst, scalar1=-math.pi, scalar2=TWO_PI, op0=ALU.is_lt, op1=ALU.mult))
        dv(lambda e: e.tensor_tensor(out=dst, in0=dst, in1=mk, op=ALU.add))
        dv(lambda e: e.tensor_scalar(out=dst, in0=dst, scalar1=math.pi, scalar2=-math.pi, op0=ALU.min, op1=ALU.max))
    wrap(rs, r1, 0.0)
    wrap(rc, r1, 0.5 * math.pi)
    ac(lambda e: e.activation(out=sn, in_=rs, func=AF.Sin))
    ac(lambda e: e.activation(out=cs, in_=rc, func=AF.Sin))
    dv(lambda e: e.tensor_tensor(out=a1re, in0=mag, in1=cs, op=ALU.mult))
    dv(lambda e: e.tensor_tensor(out=a1im, in0=mag, in1=sn, op=ALU.mult))
    dv(lambda e: e.tensor_scalar(out=nre, in0=a1re, scalar1=-1.0, scalar2=None, op0=ALU.add))
    dv(lambda e: e.tensor_tensor(out=den, in0=lre, in1=lre, op=ALU.mult))
    dv(lambda e: e.tensor_tensor(out=t1, in0=aim, in1=aim, op=ALU.mult))
    dv(lambda e: e.tensor_tensor(out=den, in0=den, in1=t1, op=ALU.add))
    dv(lambda e: e.reciprocal(out=rden, in_=den))
    dv(lambda e: e.tensor_tensor(out=t1, in0=nre, in1=lre, op=ALU.mult))
    dv(lambda e: e.tensor_tensor(out=t2, in0=a1im, in1=aim, op=ALU.mult))
    dv(lambda e: e.tensor_tensor(out=t1, in0=t1, in1=t2, op=ALU.add))
    dv(lambda e: e.tensor_tensor(out=cre, in0=t1, in1=rden, op=ALU.mult))
    dv(lambda e: e.tensor_tensor(out=t1, in0=a1im, in1=lre, op=ALU.mult))
    dv(lambda e: e.tensor_tensor(out=t2, in0=nre, in1=aim, op=ALU.mult))
    dv(lambda e: e.tensor_tensor(out=t1, in0=t1, in1=t2, op=ALU.subtract))
    dv(lambda e: e.tensor_tensor(out=cim, in0=t1, in1=rden, op=ALU.mult))
    bc16 = lambda a: a.unsqueeze(2).to_broadcast([128, 64, 16])
    tB = tmpA[:, 0:1024].rearrange("g (p c) -> g p c", c=16)
    dv(lambda e: e.tensor_tensor(out=bbre, in0=bre, in1=bc16(cre), op=ALU.mult))
    dv(lambda e: e.tensor_tensor(out=tB, in0=bim, in1=bc16(cim), op=ALU.mult))
    dv(lambda e: e.tensor_tensor(out=bbre, in0=bbre, in1=tB, op=ALU.subtract))
    dv(lambda e: e.tensor_tensor(out=bbim, in0=bim, in1=bc16(cre), op=ALU.mult))
    dv(lambda e: e.tensor_tensor(out=tB, in0=bre, in1=bc16(cim), op=ALU.mult))
    dv(lambda e: e.tensor_tensor(out=bbim, in0=bbim, in1=tB, op=ALU.add))
    dv(lambda e: e.memset(Pw[:, 0, 0, :], 1.0))
    dv(lambda e: e.memset(Pw[:, 0, 1, :], 0.0))
    dv(lambda e: e.tensor_copy(out=Pw[:, 1, 0, :], in_=a1re))
    dv(lambda e: e.tensor_copy(out=Pw[:, 1, 1, :], in_=a1im))
    for l in range(1, 8):
        dv(lambda e, l=l: e.tensor_tensor(out=t1, in0=Pw[:, l, 0, :], in1=a1re, op=ALU.mult))
        dv(lambda e, l=l: e.tensor_tensor(out=t2, in0=Pw[:, l, 1, :], in1=a1im, op=ALU.mult))
        dv(lambda e, l=l: e.tensor_tensor(out=Pw[:, l + 1, 0, :], in0=t1, in1=t2, op=ALU.subtract))
        dv(lambda e, l=l: e.tensor_tensor(out=t1, in0=Pw[:, l, 0, :], in1=a1im, op=ALU.mult))
        dv(lambda e, l=l: e.tensor_tensor(out=t2, in0=Pw[:, l, 1, :], in1=a1re, op=ALU.mult))
        dv(lambda e, l=l: e.tensor_tensor(out=Pw[:, l + 1, 1, :], in0=t1, in1=t2, op=ALU.add))
    for l in range(1, 9):
        ac(lambda e, l=l: e.activation(out=e2, in_=xx, func=AF.Exp, scale=-2.0 * l))
        dv(lambda e, l=l: e.tensor_tensor(out=Pn[:, l, 0, :], in0=Pw[:, l, 0, :], in1=e2, op=ALU.mult))
        dv(lambda e, l=l: e.scalar_tensor_tensor(out=Pn[:, l, 1, :], in0=Pw[:, l, 1, :], scalar=-1.0,
                                                 op0=ALU.mult, in1=e2, op1=ALU.mult))
    BmD, BshD, CmD = C.BmD, C.BshD, C.CmD
    scnt = [0]

    def slice_out(dram_ap, n):
        i = scnt[0] % 2
        scnt[0] += 1
        return sl[i][:, 0:n], ("sl", i), dram_ap

    for tau in range(8):
        o, okey, dst = slice_out(BmD[:, tau * 16:(tau + 1) * 16, :].rearrange("g q n -> g (q n)"), 2048)
        ov = o.rearrange("g (c r p) -> g c r p", c=16, r=2)
        tv = tmpA[:, 0:1024].rearrange("g (c p) -> g c p", c=16)
        pre = Pw[:, 7 - tau, 0, :].unsqueeze(1).to_broadcast([128, 16, 64])
        pim = Pw[:, 7 - tau, 1, :].unsqueeze(1).to_broadcast([128, 16, 64])
        xre = bbre.rearrange("g p c -> g c p")
        xim = bbim.rearrange("g p c -> g c p")
        R, W = K0 + [okey], K0 + [okey]
        tt(ov[:, :, 0, :], pre, xre, ALU.mult, R, W)
        tt(tv, pim, xim, ALU.mult, R, W)
        tt(ov[:, :, 0, :], ov[:, :, 0, :], tv, ALU.subtract, R, W)
        tt(ov[:, :, 1, :], pre, xim, ALU.mult, R, W)
        tt(tv, pim, xre, ALU.mult, R, W)
        tt(ov[:, :, 1, :], ov[:, :, 1, :], tv, ALU.add, R, W)
        P.dma(SP, dst, o, reads=[okey], writes=["BmD"])
    for which in ("bsh", "cm"):
        Dd = BshD if which == "bsh" else CmD
        Pt = Pn if which == "bsh" else Pw
        for ri in range(2):
            for phf in range(2):
                p0 = phf * 32
                o, okey, dst = slice_out(Dd[:, ri * 64 + p0: ri * 64 + p0 + 32, :].rearrange("g q n -> g (q n)"), 4096)
                ov = o.rearrange("g (p l c) -> g p l c", p=32, l=8)
                tv = tmpA.rearrange("g (p l c) -> g p l c", p=32, l=8)
                pv = lambda r_: Pt[:, 1:9, r_, p0:p0 + 32].rearrange("g l p -> g p l").unsqueeze(3) \
                    .to_broadcast([128, 32, 8, 16])
                if which == "bsh":
                    xv = lambda a: a[:, p0:p0 + 32, :].unsqueeze(2).to_broadcast([128, 32, 8, 16])
                    xre, xim = xv(bbre), xv(bbim)
                else:
                    xv = lambda a: a[:, :, p0:p0 + 32].rearrange("g c p -> g p c").unsqueeze(2) \
                        .to_broadcast([128, 32, 8, 16])
                    xre, xim = xv(cpre), xv(cpim)
                R, W = K0 + [okey], K0 + [okey]
                if ri == 0:
                    tt(ov, pv(0), xre, ALU.mult, R, W)
                    tt(tv, pv(1), xim, ALU.mult, R, W)
                    tt(ov, ov, tv, ALU.subtract, R, W)
                else:
                    tt(ov, pv(0), xim, ALU.mult, R, W)
                    tt(tv, pv(1), xre, ALU.mult, R, W)
                    if which == "bsh":
                        tt(ov, ov, tv, ALU.add, R, W)
                    else:
                        P.op(DVE, lambda e, ov=ov, tv=tv: e.scalar_tensor_tensor(
                            out=ov, in0=ov, scalar=-1.0, op0=ALU.mult, in1=tv, op1=ALU.subtract), reads=R, writes=W)
                P.dma(SP, dst, o, reads=[okey], writes=[which + "D"])
    A8 = C.A8
    dv(lambda e: e.tensor_copy(out=Pdup[:, 0, 0:64], in_=Pw[:, 8, 0, :]))
    dv(lambda e: e.tensor_copy(out=Pdup[:, 0, 64:128], in_=Pw[:, 8, 0, :]))
    dv(lambda e: e.tensor_copy(out=Pdup[:, 1, 0:64], in_=Pw[:, 8, 1, :]))
    dv(lambda e: e.tensor_copy(out=Pdup[:, 1, 64:128], in_=Pw[:, 8, 1, :]))
    dv(lambda e: e.tensor_copy(out=Drep.rearrange("g (t c) -> g t c", t=8),
                               in_=dg.unsqueeze(1).to_broadcast([128, 8, 16])))
    for i, src in enumerate((Pdup[:, 0, :], Pdup[:, 1, :], Drep)):
        P.op(PE, lambda e, i=i, src=src: e.transpose(out=C.ps[i][:, 0:128], in_=src, identity=C.identf),
             reads=K0 + ["identf"], writes=[("ps", i)])
    for hf in range(2):
        psv = lambda i: C.ps[i][hf * 64:(hf + 1) * 64, 0:128].rearrange("p (gp e) -> p gp e", e=2)[:, :, hf]
        rows = slice(hf * 64, (hf + 1) * 64)
        p0v, p1v = psv(0), psv(1)
        P.op(ACT, lambda e, p0v=p0v, rows=rows: e.copy(out=A8[rows, 0, 0:64], in_=p0v), reads=[("ps", 0)], writes=["A8"])
        P.op(ACT, lambda e, p0v=p0v, rows=rows: e.copy(out=A8[rows, 0, 64:128], in_=p0v), reads=[("ps", 0)], writes=["A8"])
        P.op(ACT, lambda e, p1v=p1v, rows=rows: e.activation(out=A8[rows, 1, 0:64], in_=p1v, func=AF.Copy, scale=-1.0),
             reads=[("ps", 1)], writes=["A8"])
        P.op(ACT, lambda e, p1v=p1v, rows=rows: e.copy(out=A8[rows, 1, 64:128], in_=p1v), reads=[("ps", 1)], writes=["A8"])
    P.op(ACT, lambda e: e.copy(out=C.Dp, in_=C.ps[2][:, 0:128]), reads=[("ps", 2)], writes=["Dp"])
    P.barrier()

    Xall = V(S, [128, 128, 128], BF16)
    Xv = Xall.rearrange("j g (t c) -> j g t c", t=8)
    cnt = 0
    for ct in range(4):
        wv, wkey = C.ws.take()
        for tau in range(8):
            pb = cnt % 4
            cnt += 1
            for k in range(16):
                P.op(PE, lambda e, pb=pb, wv=wv, k=k, tau=tau: e.matmul(
                    C.ps[pb], lhsT=C.hnT[:, k, tau * 128:(tau + 1) * 128], rhs=wv[:, k, :],
                    start=(k == 0), stop=(k == 15)), reads=[wkey, "hnT"], writes=[("ps", pb)])
            P.op(ACT, lambda e, pb=pb, ct=ct, tau=tau: e.copy(
                out=Xv[:, ct * 32:(ct + 1) * 32, tau, :], in_=C.ps[pb].rearrange("j (g c) -> j g c", c=16)),
                reads=[("ps", pb)], writes=["Xall"])
    P.barrier()
    Uall = V(RH, [128, 128, 128], BF16)
    for k in range(16):
        pb = 4 + k % 2
        for gq in range(8):
            g = k * 8 + gq
            P.op(PE, lambda e, pb=pb, g=g, gq=gq: e.transpose(
                out=C.psb[pb][:, gq * 128:(gq + 1) * 128], in_=Xall[:, g, :], identity=C.ident),
                reads=["Xall", "ident"], writes=[("ps", pb)])
        eng = ACT if k % 2 == 0 else DVE
        if eng == ACT:
            P.op(ACT, lambda e, pb=pb, k=k: e.copy(out=Uall[:, k * 8:(k + 1) * 8, :].rearrange("q g j -> q (g j)"),
                                                   in_=C.psb[pb]), reads=[("ps", pb)], writes=[("U", k)])
        else:
            P.op(DVE, lambda e, pb=pb, k=k: e.tensor_copy(out=Uall[:, k * 8:(k + 1) * 8, :].rearrange("q g j -> q (g j)"),
                                                          in_=C.psb[pb]), reads=[("ps", pb)], writes=[("U", k)])
    P.barrier()
    Bm = V(S, [128, 128, 128], BF16)
    BmDv = BmD.rearrange("g q n -> q g n")
    for hf in range(2):
        P.dma(POOL, Bm[:, hf * 64:(hf + 1) * 64, :], BmDv[:, hf * 64:(hf + 1) * 64, :], reads=["BmD"], writes=["Bm"])
    Zs = V(RM, [128, 2, 64, 128], F32)
    for gp2 in range(32):
        pb = gp2 % 4
        for gl in range(2):
            gp = gp2 * 2 + gl
            for e_ in range(2):
                g = gp * 2 + e_
                for ri in range(2):
                    P.op(PE, lambda e, pb=pb, g=g, e_=e_, ri=ri, gl=gl: e.matmul(
                        C.ps[pb][e_ * 64:(e_ + 1) * 64, (gl * 2 + ri) * 128:(gl * 2 + ri + 1) * 128],
                        lhsT=Bm[:, g, ri * 64:(ri + 1) * 64], rhs=Uall[:, g, :], start=True, stop=True),
                        reads=["Bm", ("U", g // 8)], writes=[("ps", pb)])
        zo = Zs[:, :, gp2 * 2:gp2 * 2 + 2, :]
        zi = C.ps[pb].rearrange("q (gl ri j) -> q ri gl j", gl=2, ri=2)
        if gp2 % 2 == 0:
            P.op(ACT, lambda e, zo=zo, zi=zi: e.copy(out=zo, in_=zi), reads=[("ps", pb)], writes=["Zs"])
        else:
            P.op(DVE, lambda e, zo=zo, zi=zi: e.tensor_copy(out=zo, in_=zi), reads=[("ps", pb)], writes=["Zs"])
    P.barrier()
    Sbf = V(S, [128, 2, 64, 128], BF16)
    scr = V(S + 32768, [128, 8, 128], F32)
    Sa, Sb_, T1, T2, T3 = scr[:, 0, :], scr[:, 1, :], scr[:, 2, :], scr[:, 3, :], scr[:, 4, :]
    Are2, Aim2 = A8[:, 0, :], A8[:, 1, :]
    KS = ["scan"]

    def cmul_acc(dst, src, add, ar=None, ai=None):
        ar = Are2 if ar is None else ar
        ai = Aim2 if ai is None else ai
        P.op(DVE, lambda e: e.tensor_tensor(out=T1, in0=ar, in1=src, op=ALU.mult), reads=KS + ["A8"], writes=KS)
        P.op(DVE, lambda e: e.tensor_tensor(out=T2[:, 0:64], in0=ai[:, 0:64], in1=src[:, 64:128], op=ALU.mult),
             reads=KS + ["A8"], writes=KS)
        P.op(DVE, lambda e: e.tensor_tensor(out=T2[:, 64:128], in0=ai[:, 64:128], in1=src[:, 0:64], op=ALU.mult),
             reads=KS + ["A8"], writes=KS)
        if add is not None:
            P.op(DVE, lambda e: e.tensor_tensor(out=T1, in0=T1, in1=add, op=ALU.add), reads=KS + ["Zs", "gsel"], writes=KS)
        P.op(DVE, lambda e: e.tensor_tensor(out=dst, in0=T1, in1=T2, op=ALU.add), reads=KS, writes=KS)

    if full:
        G = V(S + 36864, [128, 3, 128], F32)
        P.dma(SP, G, C.gsel.rearrange("m q n -> q m n"), writes=["gsel"])
        Q = A8[:, 2:4, :]
        P.op(DVE, lambda e: e.tensor_copy(out=Q, in_=A8[:, 0:2, :]), reads=["A8"], writes=KS)
        for it in range(7):
            P.op(DVE, lambda e: e.tensor_tensor(out=T1, in0=Q[:, 0, :], in1=Q[:, 0, :], op=ALU.mult), reads=KS, writes=KS)
            P.op(DVE, lambda e: e.tensor_tensor(out=T2, in0=Q[:, 1, :], in1=Q[:, 1, :], op=ALU.mult), reads=KS, writes=KS)
            P.op(DVE, lambda e: e.scalar_tensor_tensor(out=T3, in0=Q[:, 0, :], scalar=2.0, op0=ALU.mult,
                                                       in1=Q[:, 1, :], op1=ALU.mult), reads=KS, writes=KS)
            P.op(DVE, lambda e: e.tensor_tensor(out=Q[:, 0, :], in0=T1, in1=T2, op=ALU.subtract), reads=KS, writes=KS)
            P.op(DVE, lambda e: e.tensor_copy(out=Q[:, 1, :], in_=T3), reads=KS, writes=KS)
        P.op(DVE, lambda e: e.tensor_copy(out=Sa, in_=G[:, 0, :]), reads=["gsel"], writes=KS)
        cmul_acc(Sb_, Sa, G[:, 1, :], Q[:, 0, :], Q[:, 1, :])
        cmul_acc(Sa, Sb_, G[:, 2, :], Q[:, 0, :], Q[:, 1, :])
    else:
        P.op(DVE, lambda e: e.memset(Sa, 0.0), writes=KS)
    cur, nxt = Sa, Sb_
    Zv = Zs.rearrange("q r g j -> q (r g) j")
    Sv = Sbf.rearrange("q r g j -> q (r g) j")
    for j in range(128):
        if full:
            P.op(ACT, lambda e, cur=cur, j=j: e.copy(out=Sv[:, :, j], in_=cur), reads=KS, writes=["Sbf"])
        cmul_acc(nxt, cur, Zv[:, :, j])
        cur, nxt = nxt, cur
    if not full:
        P.dma(SP, C.send, cur, reads=KS)
        P.barrier()
        return
    P.barrier()
    Km = V(RM, [128, 128, 128], BF16)
    Cm = V(RM + 32768, [128, 2, 64, 128], BF16)
    CmDv = C.CmD.rearrange("(gp e) q n -> e q gp n", e=2)
    for e_ in range(2):
        for ri in range(2):
            P.dma(POOL, Cm[e_ * 64:(e_ + 1) * 64, ri, :, :], CmDv[e_][ri * 64:(ri + 1) * 64], reads=["cmD"], writes=["Cm"])
    bt = [V(S + 32768 + i * 8192, [128, 2, 8, 128], F32) for i in range(2)]
    mask4 = V(S + 49152, [128, 4, 128], F32)
    kt = [V(S + 51200 + i * 2048, [128, 4, 128], F32) for i in range(2)]
    P.op(DVE, lambda e: e.memset(mask4, 1.0), writes=["mask4"])
    for i in range(4):
        P.op(POOL, lambda e, i=i: e.affine_select(out=mask4[:, i, :], in_=mask4[:, i, :], compare_op=ALU.is_ge, fill=0.0,
                                                  base=15, pattern=[[16, 8], [0, 16]], channel_multiplier=-1),
             reads=["mask4"], writes=["mask4"])
    BshDv = C.BshD.rearrange("g q n -> q g n")
    CmDq = C.CmD.rearrange("g q n -> q g n")
    for gb in range(16):
        b = bt[gb % 2]
        bkey = ("bt", gb % 2)
        P.dma(SP, b[:, 0, :, :], BshDv[:, gb * 8:(gb + 1) * 8, :], reads=["bshD"], writes=[bkey])
        P.dma(SP, b[:, 1, :, :], CmDq[:, gb * 8:(gb + 1) * 8, :], reads=["cmD"], writes=[bkey])
        for hq in range(2):
            pb = (gb * 2 + hq) % 4
            kk = kt[(gb * 2 + hq) % 2]
            kkey = ("kt", (gb * 2 + hq) % 2)
            for gi in range(4):
                gl = hq * 4 + gi
                P.op(PE, lambda e, pb=pb, b=b, gl=gl, gi=gi: e.matmul(
                    C.ps[pb][:, gi * 128:(gi + 1) * 128], lhsT=b[:, 0, gl, :], rhs=b[:, 1, gl, :], start=True, stop=True),
                    reads=[bkey], writes=[("ps", pb)])
            P.op(DVE, lambda e, pb=pb, kk=kk: e.tensor_tensor(out=kk.rearrange("q a n -> q (a n)"), in0=C.ps[pb],
                                                              in1=mask4.rearrange("q a n -> q (a n)"), op=ALU.mult),
                 reads=[("ps", pb), "mask4"], writes=[kkey])
            for gi in range(4):
                g = gb * 8 + hq * 4 + gi
                P.op(DVE, lambda e, g=g, gi=gi, kk=kk: e.scalar_tensor_tensor(
                    out=Km[:, g, :], in0=C.identf, scalar=C.Dp[:, g:g + 1], op0=ALU.mult, in1=kk[:, gi, :], op1=ALU.add),
                    reads=[kkey, "Dp", "identf"], writes=["Km"])
    P.barrier()
    Yf = [V(S + 32768 + i * 4096, [128, 8, 128], F32) for i in range(2)]
    Zk = [V(S + 40960 + i * 2048, [128, 8, 128], BF16) for i in range(2)]
    zT = Uall
    for k in range(16):
        yf, ykey = Yf[k % 2], ("Yf", k % 2)
        zk, zkey = Zk[k % 2], ("Zk", k % 2)
        for hq in range(2):
            pb = hq
            for gi in range(4):
                g = k * 8 + hq * 4 + gi
                gp, e_ = g // 2, g % 2
                rows = slice(e_ * 64, (e_ + 1) * 64)
                o = C.ps[pb][:, gi * 128:(gi + 1) * 128]
                P.op(PE, lambda e, o=o, g=g: e.matmul(o, lhsT=Km[:, g, :], rhs=Uall[:, g, :], start=True, stop=False),
                     reads=["Km", ("U", k)], writes=[("ps", pb)])
                P.op(PE, lambda e, o=o, rows=rows, gp=gp: e.matmul(o, lhsT=Cm[rows, 0, gp, :], rhs=Sbf[rows, 0, gp, :],
                                                                    start=False, stop=False),
                     reads=["Cm", "Sbf"], writes=[("ps", pb)])
                P.op(PE, lambda e, o=o, rows=rows, gp=gp: e.matmul(o, lhsT=Cm[rows, 1, gp, :], rhs=Sbf[rows, 1, gp, :],
                                                                    start=False, stop=True),
                     reads=["Cm", "Sbf"], writes=[("ps", pb)])
            P.op(ACT, lambda e, pb=pb, hq=hq, yf=yf: e.copy(out=yf[:, hq * 4:(hq + 1) * 4, :].rearrange("q g j -> q (g j)"),
                                                            in_=C.ps[pb]), reads=[("ps", pb)], writes=[ykey])
        for hq in range(2):
            pb = 2 + hq
            for gi in range(4):
                P.op(PE, lambda e, pb=pb, gi=gi, hq=hq, yf=yf: e.transpose(
                    out=C.ps[pb][:, gi * 128:(gi + 1) * 128], in_=yf[:, hq * 4 + gi, :], identity=C.identf),
                    reads=[ykey, "identf"], writes=[("ps", pb)])
            P.op(ACT, lambda e, pb=pb, hq=hq, zk=zk: e.activation(
                out=zk.rearrange("j t (g c) -> j g t c", c=16)[:, hq * 4:(hq + 1) * 4, :, :],
                in_=C.ps[pb].rearrange("j (g t c) -> j g t c", g=4, t=8), func=AF.Gelu_apprx_tanh),
                reads=[("ps", pb)], writes=[zkey])
        pb = 4 + k % 2
        for t in range(8):
            P.op(PE, lambda e, pb=pb, t=t, zk=zk: e.transpose(out=C.psb[pb][:, t * 128:(t + 1) * 128], in_=zk[:, t, :],
                                                             identity=C.ident),
                 reads=[zkey, "ident"], writes=[("ps", pb)])
        P.op(DVE, lambda e, pb=pb, k=k: e.tensor_copy(out=zT[:, k * 8:(k + 1) * 8, :].rearrange("q a b -> q (a b)"),
                                                      in_=C.psb[pb]), reads=[("ps", pb)], writes=[("U", k)])
    P.barrier()
    zTk = zT.rearrange("q (k a) b -> q k (a b)", a=8)
    zgT = V(S, [128, 16, T], BF16)
    sg = [V(S + 32768 + i * 2048, [128, 512], F32) for i in range(2)]
    cnt = 0
    for mt in range(4):
        wv, wkey = C.ws.take()
        for mm in range(4):
            m = mt * 4 + mm
            for half in range(2):
                pb = cnt % 4
                sgt, skey = sg[cnt % 2], ("sg", cnt % 2)
                cnt += 1
                for k in range(16):
                    P.op(PE, lambda e, pb=pb, wv=wv, k=k, mm=mm, half=half: e.matmul(
                        C.ps[pb], lhsT=wv[:, k, mm * 128:(mm + 1) * 128], rhs=zTk[:, k, half * 512:(half + 1) * 512],
                        start=(k == 0), stop=(k == 15)), reads=[wkey, "zT"], writes=[("ps", pb)])
                P.op(ACT, lambda e, pb=pb, sgt=sgt: e.activation(out=sgt, in_=C.ps[pb], func=AF.Sigmoid),
                     reads=[("ps", pb)], writes=[skey])
                P.op(DVE, lambda e, sgt=sgt, m=m, half=half: e.tensor_tensor(
                    out=zgT[:, m, half * 512:(half + 1) * 512], in0=sgt, in1=zTk[:, m, half * 512:(half + 1) * 512],
                    op=ALU.mult), reads=[skey, "zT"], writes=["zgT"])
    P.barrier()
    for nt in range(4):
        wv, wkey = C.ws.take()
        for tb in range(8):
            pb = cnt % 4
            cnt += 1
            for k in range(16):
                P.op(PE, lambda e, pb=pb, wv=wv, k=k, tb=tb: e.matmul(
                    C.ps[pb], lhsT=zgT[:, k, tb * 128:(tb + 1) * 128], rhs=wv[:, k, :],
                    start=(k == 0), stop=(k == 15)), reads=[wkey, "zgT"], writes=[("ps", pb)])
            mvw = C.m_acc[:, tb, nt * 512:(nt + 1) * 512]
            if cnt % 2 == 0:
                P.op(ACT, lambda e, mvw=mvw, pb=pb: e.copy(out=mvw, in_=C.ps[pb]), reads=[("ps", pb)], writes=[("macc", tb)])
            else:
                P.op(DVE, lambda e, mvw=mvw, pb=pb: e.tensor_copy(out=mvw, in_=C.ps[pb]), reads=[("ps", pb)],
                     writes=[("macc", tb)])


WEIGHT_NAMES = ["s5_w_in", "s5_w_glu", "s5_w_out", "sgu_w_in", "sgu_w_out", "ffn_w_up", "ffn_w_down"]
WEIGHT_SHAPES = {"s5_w_in": [2, D, D], "s5_w_glu": [2, D, D], "s5_w_out": [2, D, D],
                 "sgu_w_in": [2, D, 2 * SH], "sgu_w_out": [2, SH, D],
                 "ffn_w_up": [DEPTH, D, FF], "ffn_w_down": [DEPTH, FF, D]}


SMALL_SHAPES = {"s5_a_re": [2, 128, 64], "s5_a_im": [2, 128, 64], "s5_log_dt": [2, 128],
                "s5_b_re": [2, 128, 64, 16], "s5_b_im": [2, 128, 64, 16],
                "s5_c_re": [2, 128, 16, 64], "s5_c_im": [2, 128, 16, 64], "s5_d": [2, D],
                "sgu_ln_g": [2, SH], "sgu_ln_b": [2, SH], "sgu_w_s": [2, 16, 128, 128], "sgu_b_s": [2, 16, 128]}


def build(phases, first=True, last=True, debug=False):
    nc = bass.Bass("TRN2", target_bir_lowering=False)
    C = Ctx()
    C.debug = debug

    def dbg(name, ap, shape, dt, reads):
        if not debug:
            return
        t = nc.dram_tensor("dbg_" + name, shape, dt, kind="ExternalOutput").ap()
        C.P.dma(SP, t, ap, reads=reads)
    C.dbg = dbg
    C.nc = nc
    C.h_in = nc.dram_tensor("h_in", [T, D], F32, kind="ExternalInput").ap()
    C.h_out = nc.dram_tensor("h_out", [T, D], F32, kind="ExternalOutput").ap()
    C.norm_g = nc.dram_tensor("norm_g", [DEPTH, 4, D], F32, kind="ExternalInput").ap()
    C.w = {}
    C.sm = {}
    used = set()
    small = set()
    for ph in phases:
        if ph["kind"] == "ffn":
            used |= {"ffn_w_up", "ffn_w_down"}
        if ph["kind"] == "sgu":
            used |= {"sgu_w_in", "sgu_w_out"}
            small |= {"sgu_ln_g", "sgu_ln_b", "sgu_w_s", "sgu_b_s"}
        if ph["kind"] in ("s5", "s5pre"):
            used |= {"s5_w_in"} | ({"s5_w_glu", "s5_w_out"} if ph["kind"] == "s5" else set())
            small |= {"s5_a_re", "s5_a_im", "s5_log_dt", "s5_b_re", "s5_b_im", "s5_c_re", "s5_c_im", "s5_d"}
    for name in SMALL_SHAPES:
        if name in small:
            C.sm[name] = nc.dram_tensor(name, SMALL_SHAPES[name], F32, kind="ExternalInput").ap()
    C.used = sorted(used) + sorted(small)
    if any(ph["kind"] == "sgu" for ph in phases):
        C.Vsp = nc.dram_tensor("Vsp", [T, SH], F32).ap()
    if any(ph["kind"] in ("s5", "s5pre") for ph in phases):
        C.BmD = nc.dram_tensor("BmD", [128, 128, 128], F32).ap()
        C.BshD = nc.dram_tensor("BshD", [128, 128, 128], F32).ap()
        C.CmD = nc.dram_tensor("CmD", [128, 128, 128], F32).ap()
    if any(ph["kind"] == "s5pre" for ph in phases):
        C.send = nc.dram_tensor("send", [128, 128], F32, kind="ExternalOutput").ap()
    if any(ph["kind"] == "s5" for ph in phases):
        C.gsel = nc.dram_tensor("gsel", [3, 128, 128], F32, kind="ExternalInput").ap()
    for name in WEIGHT_NAMES:
        if name in used:
            C.w[name] = nc.dram_tensor(name, WEIGHT_SHAPES[name], F32, kind="ExternalInput").ap()
    with ExitStack() as st:
        arena = st.enter_context(nc.sbuf_tensor("arena", [128, 204 * 1024 // 4], F32))
        C.arena = arena

        def view(off, shape, dt):
            n = 1
            for d in shape[1:]:
                n *= d
            if dt == F32:
                v = arena[:, off // 4: off // 4 + n]
            else:
                v = arena.bitcast(BF16)[:, off // 2: off // 2 + n] if dt == BF16 else \
                    arena.bitcast(dt)[:, off // 4: off // 4 + n]
            if len(shape) == 3:
                v = v.rearrange("p (a b) -> p a b", a=shape[1])
            elif len(shape) == 4:
                v = v.rearrange("p (a b c) -> p a b c", a=shape[1], b=shape[2])
            return v
        C.view = view
        C.m_acc = view(0, [128, 8, D], F32)
        C.hnT = view(65536, [128, 16, T], BF16)
        wslots = [view(98304 + i * 16384, [128, W_SLOT_ELEMS], BF16) for i in range(N_WSLOT)]
        C.SCR = 147456
        C.ident = st.enter_context(nc.sbuf_tensor("ident", [128, 128], BF16))[:]
        C.eps_t = st.enter_context(nc.sbuf_tensor("eps_t", [128, 1], F32))[:]
        C.identf = st.enter_context(nc.sbuf_tensor("identf", [128, 128], F32))[:]
        C.A8 = st.enter_context(nc.sbuf_tensor("A8", [128, 4, 128], F32))[:]
        C.Dp = st.enter_context(nc.sbuf_tensor("Dp", [128, 128], F32))[:]
        C.ps = [st.enter_context(nc.psum_tensor("ps%d" % i, [128, 512], F32))[:] for i in range(8)]
        C.psb = [C.ps[i].bitcast(BF16) for i in range(8)]
        P = Prog(nc)
        C.P = P
        C.ws = WStream(P, wslots)
        P.op(POOL, lambda e: e.memset(C.ident, 0.0), writes=["ident"])
        P.op(POOL, lambda e: e.affine_select(out=C.ident, in_=C.ident, compare_op=ALU.not_equal,
                                             fill=1.0, base=0, pattern=[[-1, 128]], channel_multiplier=1),
             reads=["ident"], writes=["ident"])
        P.op(DVE, lambda e: e.memset(C.eps_t, EPS), writes=["eps"])
        P.op(DVE, lambda e: e.tensor_copy(out=C.identf, in_=C.ident), reads=["ident"], writes=["identf"])
        P.barrier()
        for ph in phases:
            if ph["kind"] == "ffn":
                plan_ffn(C, ph)
            elif ph["kind"] == "sgu":
                plan_sgu(C, ph)
            else:
                plan_s5(C, ph)
        prev = None
        for i, ph in enumerate(phases):
            emit_boundary(C, prev if (i > 0 or not first) else None, ph)
            P.barrier()
            if ph["kind"] == "ffn":
                emit_ffn(C, ph)
            elif ph["kind"] == "sgu":
                emit_sgu(C, ph)
            else:
                emit_s5(C, ph)
            P.barrier()
            prev = ph
        if prev["kind"] != "s5pre":
            emit_boundary(C, prev, None)
        C.stats = P.finalize(st)
    return nc, C


def _phase(kind, l):
    if kind in ("s5", "s5pre"):
        return dict(kind=kind, layer=l, ord="str", gpre=(l, 0), gpost=(l, 1))
    if kind == "sgu":
        return dict(kind=kind, layer=l, ord="nat", gpre=(l, 0), gpost=(l, 1))
    return dict(kind="ffn", layer=l, ord=("str" if l % 2 == 0 else "nat"), gpre=(l, 2), gpost=(l, 3))


def _run(phases, per_core, shared, ncores=NCORE):
    nc, C = build(phases)
    names = list(C.used) + ["norm_g"]
    in_maps = []
    for c in range(ncores):
        m = {k: shared[k] for k in names}
        m.update(per_core[c])
        in_maps.append(m)
    res = run_bass_kernel_spmd(nc, in_maps, core_ids=list(range(ncores)))
    return res.results


def _gsel(sends, c):
    z = np.zeros((3, 128, 128), np.float32)
    i = c % 4
    for m in range(3):
        src = i - 3 + m
        if src >= 0:
            z[m] = sends[(c // 4) * 4 + src]
    return z


def kernel(**inputs):
    shared = {k: np.ascontiguousarray(np.asarray(v, dtype=np.float32)) for k, v in inputs.items() if k != "x"}
    x = np.asarray(inputs["x"], dtype=np.float32)
    xs = [np.ascontiguousarray(x[c // 4, (c % 4) * T:(c % 4 + 1) * T]) for c in range(NCORE)]
    ra = _run([_phase("s5pre", 0)], [dict(h_in=xs[c]) for c in range(NCORE)], shared)
    sends = [np.asarray(ra[c]["send"]) for c in range(NCORE)]
    rb = _run([_phase("s5", 0), _phase("ffn", 0), _phase("sgu", 1), _phase("ffn", 1), _phase("s5pre", 2)],
              [dict(h_in=xs[c], gsel=_gsel(sends, c)) for c in range(NCORE)], shared)
    sends = [np.asarray(rb[c]["send"]) for c in range(NCORE)]
    hs = [np.ascontiguousarray(np.asarray(rb[c]["h_out"])) for c in range(NCORE)]
    rc = _run([_phase("s5", 2), _phase("ffn", 2), _phase("sgu", 3), _phase("ffn", 3)],
              [dict(h_in=hs[c], gsel=_gsel(sends, c)) for c in range(NCORE)], shared)
    out = np.empty((2, 4 * T, D), np.float32)
    for c in range(NCORE):
        out[c // 4, (c % 4) * T:(c % 4 + 1) * T] = np.asarray(rc[c]["h_out"])
    return out
```
